# Optimizing a Trainium2 kernel written in Bass

```python
import jax, jax.numpy as jnp
from jax import lax
import numpy as np

D_MODEL = 2048
BATCH = 8
SEQ = 2048
DEPTH = 1

CHUNK = 64
EPS = 1e-6
D_A = D_MODEL // 2
A_HEADS = 8
A_DK = D_A // A_HEADS
A_DV = D_A // A_HEADS
D_B = D_MODEL // 2
CONV_W = 3
N_BRANCH = 2
IN_SIZES = (D_A, D_A, D_A, D_A, D_B, D_B, D_B, D_B)
IN_COLS = sum(IN_SIZES)

kernel_name = "hgrn2_shortconv_gated_hybrid"


def rms_norm(x, gain):
    xf = x.astype(jnp.float32)
    y = xf * lax.rsqrt(jnp.mean(xf * xf, axis=-1, keepdims=True) + EPS)
    return (y * gain.astype(jnp.float32)).astype(x.dtype)


def hgrn2_chunkwise(q, log_f, k, v):
    bsz, s, h, dk = q.shape
    dv = v.shape[-1]
    n = s // CHUNK
    q, log_f, k = (t.reshape(bsz, n, CHUNK, h, dk) for t in (q, log_f, k))
    v = v.reshape(bsz, n, CHUNK, h, dv)
    b = jnp.cumsum(log_f, axis=2)
    b_mid = b[:, :, CHUNK // 2 - 1:CHUNK // 2]
    b_last = b[:, :, CHUNK - 1:]
    q_rel = q * jnp.exp(b - b_mid)
    k_rel = k * jnp.exp(b_mid - b)
    scores = jnp.einsum('bnthd,bnshd->bnhts', q_rel, k_rel)
    causal = jnp.tril(jnp.ones((CHUNK, CHUNK), dtype=bool))
    scores = jnp.where(causal, scores, 0.0)
    o_intra = jnp.einsum('bnhts,bnshv->bnthv', scores, v)
    d_state = jnp.einsum('bnshd,bnshv->bnhdv', k * jnp.exp(b_last - b), v)
    decay = jnp.exp(b_last[:, :, 0])

    def step(state, inp):
        ds_c, dec_c = inp
        return dec_c[..., None] * state + ds_c, state

    state0 = jnp.zeros((bsz, h, dk, dv), jnp.float32)
    _, state_before = lax.scan(step, state0, (jnp.moveaxis(d_state, 1, 0), jnp.moveaxis(decay, 1, 0)))
    state_before = jnp.moveaxis(state_before, 0, 1)
    o_inter = jnp.einsum('bnthd,bnhdv->bnthv', q * jnp.exp(b), state_before)
    return (o_intra + o_inter).reshape(bsz, s, h, dv)


def causal_depthwise_conv(u, w, bias):
    s = u.shape[1]
    up = jnp.pad(u, ((0, 0), (CONV_W - 1, 0), (0, 0)))
    y = bias
    for j in range(CONV_W):
        y = y + w[j] * up[:, j:j + s]
    return y


def setup_inputs(seed: int = 0) -> dict:
    key = jax.random.key(seed)
    ks = jax.random.split(key, 16)
    f32 = jnp.float32
    nrm = lambda k, shape, scale: (jax.random.normal(k, shape, f32) * scale)
    return {
        "x": nrm(ks[0], (BATCH, SEQ, D_MODEL), 1.0),
        "c": nrm(ks[1], (BATCH, D_MODEL), 1.0),
        "w_ada": nrm(ks[2], (DEPTH, D_MODEL, 3 * D_MODEL), D_MODEL ** -0.5),
        "b_ada": nrm(ks[3], (DEPTH, 3 * D_MODEL), 0.02),
        "g_pre": 1.0 + nrm(ks[4], (DEPTH, D_MODEL), 0.02),
        "w_in": nrm(ks[5], (DEPTH, D_MODEL, IN_COLS), D_MODEL ** -0.5),
        "lb_logits": nrm(ks[6], (DEPTH + 1, D_A), 0.1),
        "g_head_a": 1.0 + nrm(ks[7], (DEPTH, A_DV), 0.02),
        "conv_w": nrm(ks[8], (DEPTH, CONV_W, D_B), CONV_W ** -0.5),
        "conv_b": nrm(ks[9], (DEPTH, D_B), 0.02),
        "w_up_a": nrm(ks[10], (DEPTH, D_A, D_MODEL), D_A ** -0.5),
        "w_up_b": nrm(ks[11], (DEPTH, D_B, D_MODEL), D_B ** -0.5),
        "w_merge": nrm(ks[12], (DEPTH, D_MODEL, N_BRANCH * D_MODEL), D_MODEL ** -0.5),
        "b_merge": nrm(ks[13], (DEPTH, N_BRANCH * D_MODEL), 0.02),
        "w_o": nrm(ks[14], (DEPTH, D_MODEL, D_MODEL), D_MODEL ** -0.5),
        "g_post": 1.0 + nrm(ks[15], (DEPTH, D_MODEL), 0.02),
    }


def reference(x, c, w_ada, b_ada, g_pre, w_in, lb_logits, g_head_a, conv_w, conv_b,
              w_up_a, w_up_b, w_merge, b_merge, w_o, g_post):
    bsz, s, _ = x.shape
    split_at = [int(v) for v in np.cumsum(IN_SIZES)[:-1]]
    lb_all = jnp.cumsum(jax.nn.softmax(lb_logits.astype(jnp.float32), axis=0), axis=0)
    c_act = jax.nn.silu(c)
    for l in range(DEPTH):
        mod = jnp.einsum('bd,de->be', c_act, w_ada[l]) + b_ada[l]
        shift, scale, gate = (m[:, None, :] for m in jnp.split(mod, 3, axis=-1))
        h = rms_norm(x, g_pre[l]) * (1.0 + scale) + shift

        proj = jnp.einsum('bsd,de->bse', h, w_in[l])
        q_a, f_a, i_a, z_a, gate_bb, gate_cc, v_b, z_b = jnp.split(proj, split_at, axis=-1)

        lb = lb_all[l]
        f = lb + (1.0 - lb) * jax.nn.sigmoid(f_a.astype(jnp.float32))
        log_f = jnp.log(f)
        k_a = 1.0 - f
        heads = lambda t: t.reshape(bsz, s, A_HEADS, -1)
        o_a = hgrn2_chunkwise(heads(q_a.astype(jnp.float32)), heads(log_f), heads(k_a),
                              heads(i_a.astype(jnp.float32)))
        o_a = rms_norm(o_a, g_head_a[l]).reshape(bsz, s, D_A).astype(x.dtype)
        y_a = o_a * jax.nn.silu(z_a)

        conv_out = causal_depthwise_conv(gate_cc * v_b, conv_w[l], conv_b[l])
        y_b = gate_bb * conv_out * jax.nn.silu(z_b)

        p_a = jnp.einsum('bse,ed->bsd', y_a, w_up_a[l])
        p_b = jnp.einsum('bse,ed->bsd', y_b, w_up_b[l])
        merge = jax.nn.sigmoid(jnp.einsum('bsd,de->bse', h, w_merge[l]) + b_merge[l])
        m_a, m_b = jnp.split(merge, N_BRANCH, axis=-1)
        out = jnp.einsum('bsd,de->bse', m_a * p_a + m_b * p_b, w_o[l])

        x = x + gate * rms_norm(out, g_post[l])
    return x
```

```python
import numpy as np
import concourse.bass as bass
import concourse.mybir as mybir
from concourse.bass_utils import run_bass_kernel_spmd

F32 = mybir.dt.float32
BF16 = mybir.dt.bfloat16
AF = mybir.ActivationFunctionType
ALU = mybir.AluOpType

P = 128
D = 2048
S = 2048
KC = 16
NH = 8
TH = 1024
NHALF = S // TH
NTB = TH // 128
NTT = TH // 512
EPS = 1e-6
NRING = 5

DEBUG = {}


class Op:
    __slots__ = ("eng", "fn", "deps", "inc", "cnt", "dma_key", "dma_val", "idx")

    def __init__(self, eng, fn):
        self.eng = eng
        self.fn = fn
        self.deps = set()
        self.inc = False
        self.cnt = 0
        self.dma_key = None
        self.dma_val = 0


class Buf:
    __slots__ = ("name", "w", "r")

    def __init__(self, name=""):
        self.name = name
        self.w = None
        self.r = {}


ENGS = ("pe", "act", "dve", "pool", "sp")


class Sched:
    def __init__(self):
        self.ops = {e: [] for e in ENGS}
        self.dma_tot = {}
        self.dma_group = set()
        self.dma_ops = []

    def add(self, eng, fn, reads=(), writes=(), dma_key=None, group=False):
        op = Op(eng, fn)
        deps = op.deps
        for b in reads:
            if b.w is not None:
                deps.add(b.w)
        for b in writes:
            if b.w is not None:
                deps.add(b.w)
            for r in b.r.values():
                deps.add(r)
        deps.discard(op)
        if dma_key is not None:
            op.dma_key = dma_key
            self.dma_tot[dma_key] = self.dma_tot.get(dma_key, 0) + 16
            op.dma_val = self.dma_tot[dma_key]
            if group:
                self.dma_group.add(dma_key)
            self.dma_ops.append(op)
        for b in reads:
            b.r[(eng, dma_key)] = op
        for b in writes:
            b.w = op
            b.r = {}
        self.ops[eng].append(op)
        return op

    def barrier(self):
        last = {e: (self.ops[e][-1] if self.ops[e] else None) for e in ENGS}
        dmas = list(self.dma_ops)
        self.dma_ops = []
        for e in ENGS:
            op = Op(e, None)
            for f in ENGS:
                if f != e and last[f] is not None:
                    op.deps.add(last[f])
            for d in dmas:
                op.deps.add(d)
            self.ops[e].append(op)


def build_program(debug=None):
    debug = debug or {}
    nc = bass.Bass("TRN2", target_bir_lowering=False)
    sc = Sched()

    def din(name, shape):
        return nc.dram_tensor(name, list(shape), F32, kind="ExternalInput").ap()

    x_d = din("x", (S, D))
    c_d = din("c_col", (P, KC))
    wada_d = din("w_ada", (12, P, KC * 512))
    bada_d = din("b_ada", (1, 3 * D))
    gpre_d = din("g_pre", (1, D))
    gpost_d = din("g_post", (1, D))
    win_d = din("w_in", (64, P, KC * 128))
    lbl_d = din("lbl", (P, 2 * NH))
    ghead_d = din("g_head", (P, 1))
    cw_d = din("conv_w", (P, 8 * 3))
    cb_d = din("conv_b", (P, 8))
    wupa_d = din("w_up_a", (16, P, 8 * 128))
    wupb_d = din("w_up_b", (16, P, 8 * 128))
    wmg_d = din("w_merge", (32, P, KC * 128))
    bmg_d = din("b_merge", (P, 32))
    wo_d = din("w_o", (P, KC * D))
    ident_d = din("ident", (P, P))
    tri_d = din("tri", (P, P))
    smask_d = din("smask", (P, TH))
    out_d = nc.dram_tensor("out", [S, D], F32, kind="ExternalOutput").ap()
    dbg_d = {}
    for name, (shape, dt) in debug.items():
        dbg_d[name] = nc.dram_tensor("dbg_" + name, list(shape), dt, kind="ExternalOutput").ap()

    avail = nc.sbuf_bytes_remaining
    ARENA_BYTES = (avail // 64) * 64 - 64
    arena_cm = nc.sbuf_tensor("arena", [P, ARENA_BYTES // 4], F32)
    psum_cm = nc.psum_tensor("ps", [P, 4096], F32)
    arena = arena_cm.__enter__()
    ps = psum_cm.__enter__()

    class Bump:
        def __init__(self, start, end):
            self.start, self.cur, self.end = start, start, end

        def take(self, nbytes, dtype, shape=None, pat=None, **kw):
            req = nbytes
            nbytes = (nbytes + 31) // 32 * 32
            assert self.cur + nbytes <= self.end, ("arena overflow", self.cur, nbytes, self.end)
            a = arena[:, self.cur // 4:(self.cur + nbytes) // 4]
            self.cur += nbytes
            if dtype == BF16:
                a = a.bitcast(BF16)[:, 0:req // 2]
            else:
                a = a[:, 0:req // 4]
            if pat is not None:
                a = a.rearrange(pat, **kw)
            return a

        def reset(self, to=None):
            self.cur = self.start if to is None else to

    top = Bump(0, ARENA_BYTES)
    hT = top.take(KC * TH * 2, BF16, pat="p (k t) -> p k t", k=KC)
    yT = top.take(KC * TH * 2, BF16, pat="p (k t) -> p k t", k=KC)
    wo_off = top.cur
    WO = top.take(KC * D * 2, BF16, pat="p (k n) -> p k n", k=KC)
    ident_f = top.take(P * 4, F32)
    ident_b = top.take(P * 2, BF16)
    tri_f = top.take(P * 4, F32)
    ones_f = top.take(P * 4, F32)
    smask = top.take(TH * 4, F32)
    gg_bc = top.take(D * 4, F32)
    gsh_col = top.take(32 * 4, F32)
    Sst = top.take(NH * 128 * 4, F32, pat="p (h v) -> p h v", h=NH)
    halo = top.take(8 * 2 * 4, F32, pat="p (g t) -> p g t", g=8)
    lbl = top.take(2 * NH * 4, F32)
    lb = top.take(NH * 4, F32)
    oml = top.take(NH * 4, F32)
    noml = top.take(NH * 4, F32)
    ghead = top.take(4, F32)
    cw = top.take(24 * 4, F32, pat="p (g j) -> p g j", g=8)
    cb = top.take(8 * 4, F32)
    bm_col = top.take(32 * 4, F32)
    c_f = top.take(KC * 4, F32)
    c_bf = top.take(KC * 2, BF16)
    stat = top.take(64 * 4, F32)
    work_off = top.cur
    work = Bump(work_off, ARENA_BYTES)
    wo_scr = Bump(wo_off, wo_off + KC * D * 2)
    hy_scr = Bump(0, 2 * KC * TH * 2)

    def bank(b, n=1):
        return ps[:, b * 512:(b + n) * 512]

    PB = [Buf("psb%d" % i) for i in range(8)]

    def dma(q, out, in_, key, reads=(), writes=(), group=False):
        if q == "pool":
            return sc.add("pool", lambda e, o=out, i=in_: e.dma_start(out=o, in_=i), reads, writes, dma_key=key, group=group)
        return sc.add("sp", lambda e, o=out, i=in_: e.dma_start(out=o, in_=i), reads, writes, dma_key=key, group=group)

    def mm(out, lhsT, rhs, start, stop, reads, writes):
        return sc.add("pe", lambda e, o=out, l=lhsT, r=rhs, s0=start, s1=stop: e.matmul(o, l, r, start=s0, stop=s1), reads, writes)

    def tr(out, in_, ident, reads, writes):
        return sc.add("pe", lambda e, o=out, i=in_, d=ident: e.transpose(o, i, d), reads, writes)

    def act(out, in_, func, reads, writes, bias=None, scale=None):
        kw = {}
        if bias is not None:
            kw["bias"] = bias
        if scale is not None:
            kw["scale"] = scale
        return sc.add("act", lambda e, o=out, i=in_, f=func, k=kw: e.activation(o, i, f, **k), reads, writes)

    def ts(eng, out, in0, s1, s2, op0, op1, reads, writes):
        if s2 is None:
            return sc.add(eng, lambda e, o=out, i=in0, a=s1, p0=op0: e.tensor_scalar(o, i, a, None, p0), reads, writes)
        return sc.add(eng, lambda e, o=out, i=in0, a=s1, b=s2, p0=op0, p1=op1: e.tensor_scalar(o, i, a, b, p0, p1), reads, writes)

    def tt(eng, out, in0, in1, op, reads, writes):
        return sc.add(eng, lambda e, o=out, a=in0, b=in1, p=op: e.tensor_tensor(o, a, b, p), reads, writes)

    def stt(out, in0, scalar, in1, op0, op1, reads, writes):
        return sc.add("dve", lambda e, o=out, a=in0, s=scalar, b=in1, p0=op0, p1=op1: e.scalar_tensor_tensor(o, a, s, b, p0, p1), reads, writes)

    def cp(eng, out, in_, reads, writes):
        return sc.add(eng, lambda e, o=out, i=in_: e.tensor_copy(o, i), reads, writes)

    CONST = Buf("const")
    for dst, src in ((ident_f, ident_d), (tri_f, tri_d), (smask, smask_d), (lbl, lbl_d), (ghead, ghead_d),
                     (cw.rearrange("p g j -> p (g j)"), cw_d), (cb, cb_d), (bm_col, bmg_d), (c_f, c_d)):
        dma("sp", dst, src, "const", writes=[CONST], group=True)
    bada_row = hy_scr.take(3 * D * 4, F32)
    gpre_row = hy_scr.take(D * 4, F32)
    gpost_row = hy_scr.take(D * 4, F32)
    gs_row = hy_scr.take(D * 4, F32)
    gg_row = hy_scr.take(D * 4, F32)
    dma("sp", bada_row[0:1, :], bada_d, "const", writes=[CONST], group=True)
    dma("sp", gpre_row[0:1, :], gpre_d, "const", writes=[CONST], group=True)
    dma("sp", gpost_row[0:1, :], gpost_d, "const", writes=[CONST], group=True)
    wada_slot = [wo_scr.take(KC * 512 * 2, BF16, pat="p (k n) -> p k n", k=KC) for _ in range(2)]
    WA = [Buf("wada0"), Buf("wada1")]
    mod_row = wo_scr.take(3 * D * 4, F32)
    MOD = Buf("mod")
    SMALL = Buf("small")

    sc.add("dve", lambda e: e.memset(ones_f, 1.0), writes=[SMALL])
    sc.add("dve", lambda e: e.memset(Sst.rearrange("p h v -> p (h v)"), 0.0), writes=[SMALL])
    sc.add("dve", lambda e: e.memset(halo.rearrange("p g t -> p (g t)"), 0.0), writes=[SMALL])
    cp("dve", ident_b, ident_f, [CONST], [SMALL])
    act(c_bf, c_f, AF.Silu, [CONST], [SMALL])
    tt("dve", lb, lbl[:, 0:NH], lbl[:, NH:2 * NH], ALU.subtract, [CONST], [SMALL])
    act(lb, lb, AF.Sigmoid, [SMALL], [SMALL])
    ts("dve", oml, lb, -1.0, 1.0, ALU.mult, ALU.add, [SMALL], [SMALL])
    ts("dve", noml, oml, -1.0, None, ALU.mult, None, [SMALL], [SMALL])

    for n in range(12):
        s = n % 2
        dma("pool", wada_slot[s].rearrange("p k n -> p (k n)"), wada_d[n], "wada%d" % s, writes=[WA[s]])
        for k in range(KC):
            mm(bank(s)[0:1, :], c_bf[:, k:k + 1], wada_slot[s][:, k, :], k == 0, k == KC - 1,
               [SMALL, WA[s]], [PB[s]])
        tt("dve", mod_row[0:1, n * 512:(n + 1) * 512], bank(s)[0:1, :], bada_row[0:1, n * 512:(n + 1) * 512],
           ALU.add, [PB[s], CONST], [MOD])
    shift_row = mod_row[0:1, 0:D]
    scale_row = mod_row[0:1, D:2 * D]
    gate_row = mod_row[0:1, 2 * D:3 * D]
    stt(gs_row[0:1, :], scale_row, 1.0, gpre_row[0:1, :], ALU.add, ALU.mult, [MOD, CONST], [MOD])
    tt("dve", gg_row[0:1, :], gate_row, gpost_row[0:1, :], ALU.mult, [MOD, CONST], [MOD])
    for k in range(KC):
        mm(bank(2)[:, k:k + 1], gs_row[0:1, k * 128:(k + 1) * 128], ones_f[0:1, 0:1], True, True, [MOD, SMALL], [PB[2]])
    for k in range(KC):
        mm(bank(2)[:, 16 + k:17 + k], shift_row[0:1, k * 128:(k + 1) * 128], ones_f[0:1, 0:1], True, True,
           [MOD, SMALL], [PB[2]])
    cp("dve", gsh_col, bank(2)[:, 0:32], [PB[2]], [SMALL])
    for n in range(4):
        mm(bank(4 + n), ones_f[0:1, :], gg_row[0:1, n * 512:(n + 1) * 512], True, True, [MOD, SMALL], [PB[4 + n]])
    cp("dve", gg_bc, ps[:, 4 * 512:8 * 512], [PB[4], PB[5], PB[6], PB[7]], [SMALL])
    sc.barrier()

    if "gsh" in dbg_d:
        dma("sp", dbg_d["gsh"], gsh_col, "dbg_gsh", reads=[SMALL])
    if "gg" in dbg_d:
        dma("sp", dbg_d["gg"], gg_bc, "dbg_gg", reads=[SMALL])

    WOB = Buf("wo")

    def load_wo():
        for q in range(4):
            dma("pool", WO[:, 4 * q:4 * q + 4, :].rearrange("p k n -> p (k n)"),
                wo_d[:, 4 * q * D:(4 * q + 4) * D], "wo", writes=[WOB], group=True)

    ring_off = work.cur
    ring = [work.take(KC * 128 * 2, BF16, pat="p (k n) -> p k n", k=KC) for _ in range(NRING)]
    RB = [Buf("ring%d" % i) for i in range(NRING)]
    ring_pos = [0]
    phase_off = work.cur

    def wload(src_list):
        i = ring_pos[0] % NRING
        ring_pos[0] += 1
        for (k0, k1, src) in src_list:
            dma("pool", ring[i][:, k0:k1, :].rearrange("p k n -> p (k n)"), src, "ring%d" % i, writes=[RB[i]])
        return ring[i], RB[i]

    HT = [Buf("hT%d" % t) for t in range(NTT)]
    YT = [Buf("yT%d" % g) for g in range(16)]

    for h in range(NHALF):
        t0 = h * TH
        work.reset(phase_off)
        xt = [work.take(D * 4, F32) for _ in range(2)]
        xn = [work.take(D * 4, F32) for _ in range(2)]
        junk = work.take(D * 2, BF16)
        XT = [Buf("xt0"), Buf("xt1")]
        XN = [Buf("xn0"), Buf("xn1")]
        JK = Buf("junk")
        ST = [Buf("st0"), Buf("st1")]
        for tb in range(NTB):
            s = tb % 2
            r0 = t0 + tb * 128
            dma("sp", xt[s], x_d[r0:r0 + 128, :], "xt%d" % s, writes=[XT[s]])
            ssq = stat[:, 4 * s:4 * s + 1]
            rt = stat[:, 4 * s + 1:4 * s + 2]
            rstd = stat[:, 4 * s + 2:4 * s + 3]
            sc.add("act", lambda e, o=junk, i=xt[s], a=ssq: e.activation(o, i, AF.Square, accum_out=a), [XT[s]], [JK, ST[s]])
            act(rt, ssq, AF.Sqrt, [ST[s]], [ST[s]], bias=EPS, scale=1.0 / D)
            sc.add("dve", lambda e, o=rstd, i=rt: e.reciprocal(o, i), [ST[s]], [ST[s]])
            ts("dve", xn[s], xt[s], rstd, None, ALU.mult, None, [XT[s], ST[s]], [XN[s]])
            pb = [PB[4 * s + i] for i in range(4)]
            for k in range(KC):
                tr(ps[:, (4 * s) * 512 + k * 128:(4 * s) * 512 + (k + 1) * 128], xn[s][:, k * 128:(k + 1) * 128], ident_f,
                   [XN[s], SMALL], [pb[k // 4]])
            for k in range(KC):
                src = ps[:, (4 * s) * 512 + k * 128:(4 * s) * 512 + (k + 1) * 128]
                dst = hT[:, k, tb * 128:(tb + 1) * 128]
                if k % 2 == 0:
                    act(dst, src, AF.Identity, [pb[k // 4], SMALL], [HT[tb // 4]],
                        bias=gsh_col[:, 16 + k:17 + k], scale=gsh_col[:, k:k + 1])
                else:
                    ts("dve", dst, src, gsh_col[:, k:k + 1], gsh_col[:, 16 + k:17 + k], ALU.mult, ALU.add,
                       [pb[k // 4], SMALL], [HT[tb // 4]])
        sc.barrier()
        if h == 0 and "hT" in dbg_d:
            dma("sp", dbg_d["hT"], hT.rearrange("p k t -> p (k t)"), "dbg_hT", reads=HT)

        work.reset(phase_off)
        T = [work.take(TH * 4, F32) for _ in range(5)]
        TB_ = [Buf("T%d" % i) for i in range(5)]
        QR = work.take(TH * 2, BF16)
        KR = work.take(TH * 2, BF16)
        KTm = work.take(TH * 2, BF16, pat="p (c d) -> p c d", c=NTB)
        VTm = work.take(TH * 2, BF16, pat="p (c d) -> p c d", c=NTB)
        STm = work.take(TH * 2, BF16, pat="p (c d) -> p c d", c=NTB)
        SBs = [work.take(128 * 2, BF16) for _ in range(2)]
        TMP = work.take(128 * 4, F32)
        cst = work.take(5 * NTB * 4, F32)
        QRb, KRb, KTb, VTb, STb = Buf("QR"), Buf("KR"), Buf("KT"), Buf("VT"), Buf("ST")
        SBb = [Buf("SB0"), Buf("SB1")]
        TMPb, CSTb, SSb = Buf("TMP"), Buf("cst"), Buf("S")
        bmid = cst[:, 0:NTB]
        blast = cst[:, NTB:2 * NTB]
        emid = cst[:, 2 * NTB:3 * NTB]
        Edec = cst[:, 3 * NTB:4 * NTB]
        dec = cst[:, 4 * NTB:5 * NTB]
        psK = bank(7).bitcast(BF16)

        for hd in range(NH):
            if h == 0 and hd == 2:
                load_wo()
            Wq, Bq = wload([(0, KC, win_d[hd])])
            Wf, Bf = wload([(0, KC, win_d[8 + hd])])
            Wi, Bi = wload([(0, KC, win_d[16 + hd])])
            Wz, Bz = wload([(0, KC, win_d[24 + hd])])
            col = slice(hd, hd + 1)

            def proj_fm(W, Bw, b0):
                for t in range(NTT):
                    for k in range(KC):
                        mm(bank(b0 + t), W[:, k, :], hT[:, k, t * 512:(t + 1) * 512], k == 0, k == KC - 1,
                           [Bw, HT[t]], [PB[b0 + t]])

            proj_fm(Wf, Bf, 0)
            act(T[0], bank(0, 2), AF.Sigmoid, [PB[0], PB[1]], [TB_[0]])
            proj_fm(Wq, Bq, 2)
            act(T[4], bank(2, 2), AF.Copy, [PB[2], PB[3]], [TB_[4]])
            act(T[1], T[0], AF.Ln, [TB_[0], SMALL], [TB_[1]], bias=lb[:, col], scale=oml[:, col])
            ts("dve", T[3], T[0], noml[:, col], oml[:, col], ALU.mult, ALU.add, [TB_[0], SMALL], [TB_[3]])
            sc.add("dve", lambda e, o=T[2], i=T[1]: e.tensor_tensor_scan(o, smask, i, 0.0, ALU.mult, ALU.add),
                   [TB_[1], CONST], [TB_[2]])
            b3 = T[2].rearrange("p (c t) -> p c t", c=NTB)
            cp("dve", bmid.unsqueeze(2), b3[:, :, 63:64], [TB_[2]], [CSTb])
            cp("dve", blast.unsqueeze(2), b3[:, :, 127:128], [TB_[2]], [CSTb])
            tt("dve", T[1].rearrange("p (c t) -> p c t", c=NTB), b3, b3[:, :, 63:64].broadcast_to([P, NTB, 128]),
               ALU.subtract, [TB_[2]], [TB_[1]])
            act(T[0], T[1], AF.Exp, [TB_[1]], [TB_[0]])
            act(T[2], T[1], AF.Exp, [TB_[1], CSTb], [TB_[2]], scale=-1.0)
            act(emid, bmid, AF.Exp, [CSTb], [CSTb])
            act(dec, blast, AF.Exp, [CSTb], [CSTb])
            tt("dve", Edec, blast, bmid, ALU.subtract, [CSTb], [CSTb])
            act(Edec, Edec, AF.Exp, [CSTb], [CSTb])
            tt("dve", QR, T[4], T[0], ALU.mult, [TB_[4], TB_[0]], [QRb])
            tt("dve", KR, T[3], T[2], ALU.mult, [TB_[3], TB_[2]], [KRb])
            proj_fm(Wz, Bz, 0)
            act(T[4], bank(0, 2), AF.Silu, [PB[0], PB[1]], [TB_[4]])
            for c in range(NTB):
                for k in range(KC):
                    mm(ps[:, 2 * 512 + c * 128:2 * 512 + (c + 1) * 128], hT[:, k, c * 128:(c + 1) * 128], Wi[:, k, :],
                       k == 0, k == KC - 1, [Bi, HT[c // 4]], [PB[2 + c // 4]])
            cp("dve", VTm.rearrange("p c d -> p (c d)"), bank(2, 2), [PB[2], PB[3]], [VTb])
            for c in range(NTB):
                tr(psK[:, c * 128:(c + 1) * 128], KR[:, c * 128:(c + 1) * 128], ident_b, [KRb, SMALL], [PB[7]])
            act(KTm.rearrange("p c d -> p (c d)"), psK, AF.Copy, [PB[7]], [KTb])
            for c in range(NTB):
                mm(ps[:, 4 * 512 + c * 128:4 * 512 + (c + 1) * 128], KR[:, c * 128:(c + 1) * 128],
                   QR[:, c * 128:(c + 1) * 128], True, True, [KRb, QRb], [PB[4 + c // 4]])
            tt("dve", STm, bank(4, 2).rearrange("p (c t) -> p c t", c=NTB),
               tri_f.unsqueeze(1).broadcast_to([P, NTB, 128]), ALU.mult, [PB[4], PB[5], CONST], [STb])
            for c in range(NTB):
                mm(ps[:, 6 * 512 + c * 128:6 * 512 + (c + 1) * 128], KTm[:, c, :], VTm[:, c, :], True, True,
                   [KTb, VTb], [PB[6 + c // 4]])
            for c in range(NTB):
                sb = c % 2
                ts("dve", SBs[sb], Sst[:, hd, :], emid[:, c:c + 1], None, ALU.mult, None, [SSb, CSTb], [SBb[sb]])
                o_ap = ps[:, 4 * 512 + c * 128:4 * 512 + (c + 1) * 128]
                mm(o_ap, VTm[:, c, :], STm[:, c, :], True, False, [VTb, STb], [PB[4 + c // 4]])
                mm(o_ap, SBs[sb], QR[:, c * 128:(c + 1) * 128], False, True, [SBb[sb], QRb], [PB[4 + c // 4]])
                ts("dve", TMP, ps[:, 6 * 512 + c * 128:6 * 512 + (c + 1) * 128], Edec[:, c:c + 1], None, ALU.mult, None,
                   [PB[6 + c // 4], CSTb], [TMPb])
                stt(Sst[:, hd, :], Sst[:, hd, :], dec[:, c:c + 1], TMP, ALU.mult, ALU.add, [SSb, TMPb, CSTb], [SSb])
            act(T[0], bank(4, 2), AF.Square, [PB[4], PB[5]], [TB_[0]])
            for t in range(NTT):
                mm(bank(6 + t), ones_f, T[0][:, t * 512:(t + 1) * 512], True, True, [TB_[0], SMALL], [PB[6 + t]])
            act(T[1], bank(6, 2), AF.Sqrt, [PB[6], PB[7]], [TB_[1]], bias=EPS, scale=1.0 / 128)
            sc.add("dve", lambda e, o=T[1]: e.reciprocal(o, o), [TB_[1]], [TB_[1]])
            tt("dve", T[3], bank(4, 2), T[1], ALU.mult, [PB[4], PB[5], TB_[1]], [TB_[3]])
            stt(yT[:, hd, :], T[3], ghead[:, 0:1], T[4], ALU.mult, ALU.mult, [TB_[3], TB_[4], CONST], [YT[hd]])

        U = T[1]
        Ubuf = work.take((TH + 2) * 4, F32)
        UB = Buf("U")
        for g in range(8):
            Wbb, Bbb = wload([(0, KC, win_d[32 + g])])
            Wcc, Bcc = wload([(0, KC, win_d[40 + g])])
            Wvb, Bvb = wload([(0, KC, win_d[48 + g])])
            Wzb, Bzb = wload([(0, KC, win_d[56 + g])])

            def proj_fm2(W, Bw, b0):
                for t in range(NTT):
                    for k in range(KC):
                        mm(bank(b0 + t), W[:, k, :], hT[:, k, t * 512:(t + 1) * 512], k == 0, k == KC - 1,
                           [Bw, HT[t]], [PB[b0 + t]])

            proj_fm2(Wcc, Bcc, 0)
            act(T[0], bank(0, 2), AF.Copy, [PB[0], PB[1]], [TB_[0]])
            proj_fm2(Wvb, Bvb, 2)
            cp("dve", Ubuf[:, 0:2], halo[:, g, :], [SSb], [UB])
            tt("dve", Ubuf[:, 2:TH + 2], bank(2, 2), T[0], ALU.mult, [PB[2], PB[3], TB_[0]], [UB])
            ts("dve", T[2], Ubuf[:, 2:TH + 2], cw[:, g, 2:3], cb[:, g:g + 1], ALU.mult, ALU.add, [UB, CONST], [TB_[2]])
            stt(T[2], Ubuf[:, 1:TH + 1], cw[:, g, 1:2], T[2], ALU.mult, ALU.add, [UB, TB_[2], CONST], [TB_[2]])
            stt(T[2], Ubuf[:, 0:TH], cw[:, g, 0:1], T[2], ALU.mult, ALU.add, [UB, TB_[2], CONST], [TB_[2]])
            cp("dve", halo[:, g, :], Ubuf[:, TH:TH + 2], [UB], [SSb])
            proj_fm2(Wzb, Bzb, 4)
            act(T[4], bank(4, 2), AF.Silu, [PB[4], PB[5]], [TB_[4]])
            proj_fm2(Wbb, Bbb, 6)
            tt("dve", T[3], bank(6, 2), T[4], ALU.mult, [PB[6], PB[7], TB_[4]], [TB_[3]])
            tt("dve", yT[:, 8 + g, :], T[3], T[2], ALU.mult, [TB_[3], TB_[2]], [YT[8 + g]])
        sc.barrier()
        if h == 0 and "yT" in dbg_d:
            dma("sp", dbg_d["yT"], yT.rearrange("p k t -> p (k t)"), "dbg_yT", reads=YT)

        work.reset(phase_off)
        mixT = work.take(KC * TH * 2, BF16, pat="p (k t) -> p k t", k=KC)
        MX = [Buf("mix%d" % t) for t in range(NTB)]
        MT = [[work.take(512 * 4, F32) for _ in range(4)] for _ in range(1)]
        MTb = [Buf("mt%d" % i) for i in range(4)]
        for j in range(16):
            WU, BU = wload([(0, 8, wupa_d[j]), (8, 16, wupb_d[j])])
            WMA, BMA = wload([(0, KC, wmg_d[j])])
            WMB, BMB = wload([(0, KC, wmg_d[16 + j])])
            for t in range(NTT):
                s = (j * NTT + t) % 2
                b0 = 4 * s
                tsl = slice(t * 512, (t + 1) * 512)
                for k in range(8):
                    mm(bank(b0), WU[:, k, :], yT[:, k, tsl], k == 0, k == 7, [BU] + YT[0:8], [PB[b0]])
                for k in range(8):
                    mm(bank(b0 + 1), WU[:, 8 + k, :], yT[:, 8 + k, tsl], k == 0, k == 7, [BU] + YT[8:16], [PB[b0 + 1]])
                for k in range(KC):
                    mm(bank(b0 + 2), WMA[:, k, :], hT[:, k, tsl], k == 0, k == KC - 1, [BMA, HT[t]], [PB[b0 + 2]])
                for k in range(KC):
                    mm(bank(b0 + 3), WMB[:, k, :], hT[:, k, tsl], k == 0, k == KC - 1, [BMB, HT[t]], [PB[b0 + 3]])
                act(MT[0][0], bank(b0 + 2), AF.Sigmoid, [PB[b0 + 2], CONST], [MTb[0]], bias=bm_col[:, j:j + 1])
                act(MT[0][1], bank(b0 + 3), AF.Sigmoid, [PB[b0 + 3], CONST], [MTb[1]], bias=bm_col[:, 16 + j:17 + j])
                tt("dve", MT[0][2], bank(b0), MT[0][0], ALU.mult, [PB[b0], MTb[0]], [MTb[2]])
                tt("dve", MT[0][3], bank(b0 + 1), MT[0][1], ALU.mult, [PB[b0 + 1], MTb[1]], [MTb[3]])
                tt("dve", mixT[:, j, tsl], MT[0][2], MT[0][3], ALU.add, [MTb[2], MTb[3]], [MX[4 * t + i] for i in range(4)])
        sc.barrier()
        if h == 0 and "mixT" in dbg_d:
            dma("sp", dbg_d["mixT"], mixT.rearrange("p k t -> p (k t)"), "dbg_mixT", reads=MX)

        work.reset(phase_off)
        work.take(KC * TH * 2, BF16)
        rb = Bump(ring_off, phase_off)
        xt = [rb.take(D * 4, F32) for _ in range(2)]
        Y = work.take(D * 4, F32)
        XT = [Buf("xt0b"), Buf("xt1b")]
        YB = Buf("Y")
        ST = [Buf("st0b"), Buf("st1b")]
        for tb in range(NTB):
            s = tb % 2
            r0 = t0 + tb * 128
            dma("sp", xt[s], x_d[r0:r0 + 128, :], "xt%d" % s, writes=[XT[s]])
            pb = [PB[4 * s + i] for i in range(4)]
            for n in range(4):
                for k in range(KC):
                    mm(bank(4 * s + n), mixT[:, k, tb * 128:(tb + 1) * 128], WO[:, k, n * 512:(n + 1) * 512],
                       k == 0, k == KC - 1, [MX[tb], WOB], [pb[n]])
            ssq = stat[:, 4 * s:4 * s + 1]
            rt = stat[:, 4 * s + 1:4 * s + 2]
            rstd = stat[:, 4 * s + 2:4 * s + 3]
            o_ps = ps[:, 4 * s * 512:(4 * s + 4) * 512]
            sc.add("act", lambda e, o=Y, i=o_ps, a=ssq: e.activation(o, i, AF.Square, accum_out=a), pb, [YB, ST[s]])
            act(rt, ssq, AF.Sqrt, [ST[s]], [ST[s]], bias=EPS, scale=1.0 / D)
            sc.add("dve", lambda e, o=rstd, i=rt: e.reciprocal(o, i), [ST[s]], [ST[s]])
            stt(Y, o_ps, rstd, gg_bc, ALU.mult, ALU.mult, pb + [ST[s], SMALL], [YB])
            tt("dve", xt[s], Y, xt[s], ALU.add, [YB, XT[s]], [XT[s]])
            dma("sp", out_d[r0:r0 + 128, :], xt[s], "ot%d" % s, reads=[XT[s]])
        sc.barrier()

    sc.barrier()

    import contextlib
    with contextlib.ExitStack() as es:
        eng_sems = {e: es.enter_context(nc.semaphore("sem_" + e)) for e in ("pe", "act", "dve", "pool")}
        dma_sems = {k: es.enter_context(nc.semaphore("dsem_" + k)) for k in sc.dma_tot}
        engines = {"pe": nc.tensor, "act": nc.scalar, "dve": nc.vector, "pool": nc.gpsimd, "sp": nc.sync}
        block = es.enter_context(nc.Block())

        @block.tensor
        def _(e):
            sc.emit_one(nc, "pe", e, eng_sems, dma_sems)

        @block.scalar
        def _(e):
            sc.emit_one(nc, "act", e, eng_sems, dma_sems)

        @block.vector
        def _(e):
            sc.emit_one(nc, "dve", e, eng_sems, dma_sems)

        @block.gpsimd
        def _(e):
            sc.emit_one(nc, "pool", e, eng_sems, dma_sems)

        @block.sync
        def _(e):
            sc.emit_one(nc, "sp", e, eng_sems, dma_sems)

    psum_cm.__exit__(None, None, None)
    arena_cm.__exit__(None, None, None)
    return nc


def _prepare(sc):
    for e in ENGS:
        for op in sc.ops[e]:
            for d in op.deps:
                if d.dma_key is None and d.fn is not None:
                    if d.eng == "pe" and e == "pe":
                        continue
                    d.inc = True
    for e in ENGS:
        c = 0
        for op in sc.ops[e]:
            if op.dma_key is None and op.inc:
                c += 1
            op.cnt = c


def _emit_one(self, nc, e, eng, eng_sems, dma_sems):
    if not getattr(self, "_prepared", False):
        _prepare(self)
        self._prepared = True
    waited = {}
    for op in self.ops[e]:
        need = {}
        for d in op.deps:
            if d.dma_key is not None:
                if d.dma_key == op.dma_key and d.dma_key in self.dma_group:
                    continue
                key = ("dma", d.dma_key)
                v = self.dma_tot[d.dma_key] if d.dma_key in self.dma_group else d.dma_val
            else:
                if d.eng == e and e == "pe":
                    continue
                if d.eng == "sp":
                    continue
                key = ("eng", d.eng)
                v = d.cnt
            if v > need.get(key, 0):
                need[key] = v
        for key, v in need.items():
            if v <= 0 or waited.get(key, 0) >= v:
                continue
            sem = dma_sems[key[1]] if key[0] == "dma" else eng_sems[key[1]]
            eng.wait_ge(sem, v)
            waited[key] = v
        if op.fn is None:
            continue
        ins = op.fn(eng)
        if op.dma_key is not None:
            ins.then_inc(dma_sems[op.dma_key], 16)
        elif op.inc:
            ins.then_inc(eng_sems[e], 1)


Sched.emit_one = _emit_one


def _layout_inputs(x, c, w_ada, b_ada, g_pre, w_in, lb_logits, g_head_a, conv_w, conv_b,
                   w_up_a, w_up_b, w_merge, b_merge, w_o, g_post):
    f = lambda a: np.ascontiguousarray(np.asarray(a, dtype=np.float32))
    shared = {
        "w_ada": f(np.asarray(w_ada)[0].reshape(KC, P, 12, 512).transpose(2, 1, 0, 3).reshape(12, P, KC * 512)),
        "b_ada": f(np.asarray(b_ada)[0].reshape(1, 3 * D)),
        "g_pre": f(np.asarray(g_pre)[0].reshape(1, D)),
        "g_post": f(np.asarray(g_post)[0].reshape(1, D)),
        "w_in": f(np.asarray(w_in)[0].reshape(KC, P, 64, 128).transpose(2, 1, 0, 3).reshape(64, P, KC * 128)),
        "lbl": f(np.asarray(lb_logits).reshape(2, NH, P).transpose(2, 0, 1).reshape(P, 2 * NH)),
        "g_head": f(np.asarray(g_head_a)[0].reshape(P, 1)),
        "conv_w": f(np.asarray(conv_w)[0].reshape(3, 8, P).transpose(2, 1, 0).reshape(P, 24)),
        "conv_b": f(np.asarray(conv_b)[0].reshape(8, P).T),
        "w_up_a": f(np.asarray(w_up_a)[0].reshape(8, P, 16, 128).transpose(2, 1, 0, 3).reshape(16, P, 8 * 128)),
        "w_up_b": f(np.asarray(w_up_b)[0].reshape(8, P, 16, 128).transpose(2, 1, 0, 3).reshape(16, P, 8 * 128)),
        "w_merge": f(np.asarray(w_merge)[0].reshape(KC, P, 32, 128).transpose(2, 1, 0, 3).reshape(32, P, KC * 128)),
        "b_merge": f(np.asarray(b_merge)[0].reshape(32, P).T),
        "w_o": f(np.asarray(w_o)[0].reshape(KC, P, D).transpose(1, 0, 2).reshape(P, KC * D)),
        "ident": np.eye(P, dtype=np.float32),
        "tri": np.triu(np.ones((P, P), dtype=np.float32)),
        "smask": f(np.tile((np.arange(128) != 0).astype(np.float32), (P, TH // 128))),
    }
    xs = np.asarray(x, dtype=np.float32)
    cs = np.asarray(c, dtype=np.float32)
    in_maps = []
    for b in range(8):
        m = dict(shared)
        m["x"] = np.ascontiguousarray(xs[b])
        m["c_col"] = f(cs[b].reshape(KC, P).T)
        in_maps.append(m)
    return in_maps


_NC_CACHE = {}


def kernel(**inputs):
    in_maps = _layout_inputs(**inputs)
    key = tuple(sorted(DEBUG.keys()))
    if key not in _NC_CACHE:
        _NC_CACHE[key] = build_program(DEBUG)
    nc = _NC_CACHE[key]
    res = run_bass_kernel_spmd(nc, in_maps, core_ids=list(range(8)))
    if DEBUG:
        kernel.last_results = res.results
    return np.stack([np.asarray(r["out"], dtype=np.float32) for r in res.results], axis=0)
```

```python
import numpy as np
import concourse.bass as bass
import concourse.mybir as mybir
from concourse.bass_utils import run_bass_kernel_spmd

F32 = mybir.dt.float32
BF16 = mybir.dt.bfloat16
AF = mybir.ActivationFunctionType
ALU = mybir.AluOpType

P = 128
D = 2048
S = 2048
KC = 16
NH = 8
TH = 1024
NHALF = S // TH
NTB = TH // 128
NTT = TH // 512
EPS = 1e-6
NRING = 6

DEBUG = {}


class Op:
    __slots__ = ("eng", "fn", "deps", "inc", "cnt", "dma_key", "dma_val", "idx")

    def __init__(self, eng, fn):
        self.eng = eng
        self.fn = fn
        self.deps = set()
        self.inc = False
        self.cnt = 0
        self.dma_key = None
        self.dma_val = 0


class Buf:
    __slots__ = ("name", "w", "r")

    def __init__(self, name=""):
        self.name = name
        self.w = None
        self.r = {}


ENGS = ("pe", "act", "dve", "pool", "sp")


class Sched:
    def __init__(self):
        self.ops = {e: [] for e in ENGS}
        self.dma_tot = {}
        self.dma_group = set()
        self.dma_ops = []

    def add(self, eng, fn, reads=(), writes=(), dma_key=None, group=False):
        op = Op(eng, fn)
        deps = op.deps
        for b in reads:
            if b.w is not None:
                deps.add(b.w)
        for b in writes:
            if b.w is not None:
                deps.add(b.w)
            for r in b.r.values():
                deps.add(r)
        deps.discard(op)
        if dma_key is not None:
            op.dma_key = dma_key
            self.dma_tot[dma_key] = self.dma_tot.get(dma_key, 0) + 16
            op.dma_val = self.dma_tot[dma_key]
            if group:
                self.dma_group.add(dma_key)
            self.dma_ops.append(op)
        for b in reads:
            b.r[(eng, dma_key)] = op
        for b in writes:
            b.w = op
            b.r = {}
        self.ops[eng].append(op)
        return op

    def barrier(self):
        last = {e: (self.ops[e][-1] if self.ops[e] else None) for e in ENGS}
        dmas = list(self.dma_ops)
        self.dma_ops = []
        for e in ENGS:
            op = Op(e, None)
            for f in ENGS:
                if f != e and last[f] is not None:
                    op.deps.add(last[f])
            for d in dmas:
                op.deps.add(d)
            self.ops[e].append(op)


def build_program(debug=None):
    debug = debug or {}
    nc = bass.Bass("TRN2", target_bir_lowering=False)
    sc = Sched()

    def din(name, shape):
        return nc.dram_tensor(name, list(shape), F32, kind="ExternalInput").ap()

    x_d = din("x", (S, D))
    c_d = din("c_col", (P, KC))
    wada_d = din("w_ada", (12, P, KC * 512))
    bada_d = din("b_ada", (1, 3 * D))
    gpre_d = din("g_pre", (1, D))
    gpost_d = din("g_post", (1, D))
    win_d = din("w_in", (64, P, KC * 128))
    lbl_d = din("lbl", (P, 2 * NH))
    ghead_d = din("g_head", (P, 1))
    cw_d = din("conv_w", (P, 8 * 3))
    cb_d = din("conv_b", (P, 8))
    wupa_d = din("w_up_a", (16, P, 8 * 128))
    wupb_d = din("w_up_b", (16, P, 8 * 128))
    wmg_d = din("w_merge", (32, P, KC * 128))
    bmg_d = din("b_merge", (P, 32))
    wo_d = din("w_o", (P, KC * D))
    ident_d = din("ident", (P, P))
    tri_d = din("tri", (P, P))
    smask_d = din("smask", (P, TH))
    out_d = nc.dram_tensor("out", [S, D], F32, kind="ExternalOutput").ap()
    dbg_d = {}
    for name, (shape, dt) in debug.items():
        dbg_d[name] = nc.dram_tensor("dbg_" + name, list(shape), dt, kind="ExternalOutput").ap()

    avail = nc.sbuf_bytes_remaining
    ARENA_BYTES = (avail // 64) * 64 - 64
    arena_cm = nc.sbuf_tensor("arena", [P, ARENA_BYTES // 4], F32)
    psum_cm = nc.psum_tensor("ps", [P, 4096], F32)
    arena = arena_cm.__enter__()
    ps = psum_cm.__enter__()

    class Bump:
        def __init__(self, start, end):
            self.start, self.cur, self.end = start, start, end

        def take(self, nbytes, dtype, shape=None, pat=None, **kw):
            req = nbytes
            nbytes = (nbytes + 31) // 32 * 32
            assert self.cur + nbytes <= self.end, ("arena overflow", self.cur, nbytes, self.end)
            a = arena[:, self.cur // 4:(self.cur + nbytes) // 4]
            self.cur += nbytes
            if dtype == BF16:
                a = a.bitcast(BF16)[:, 0:req // 2]
            else:
                a = a[:, 0:req // 4]
            if pat is not None:
                a = a.rearrange(pat, **kw)
            return a

        def reset(self, to=None):
            self.cur = self.start if to is None else to

    top = Bump(0, ARENA_BYTES)
    hT = top.take(KC * TH * 2, BF16, pat="p (k t) -> p k t", k=KC)
    yT = top.take(KC * TH * 2, BF16, pat="p (k t) -> p k t", k=KC)
    wo_off = top.cur
    WO = top.take(KC * D * 2, BF16, pat="p (k n) -> p k n", k=KC)
    ident_f = top.take(P * 4, F32)
    ident_b = top.take(P * 2, BF16)
    tri_f = top.take(P * 4, F32)
    ones_f = top.take(P * 4, F32)
    smask = top.take(TH * 4, F32)
    gg_bc = top.take(D * 4, F32)
    gsh_col = top.take(32 * 4, F32)
    Sst = top.take(NH * 128 * 4, F32, pat="p (h v) -> p h v", h=NH)
    halo = top.take(8 * 2 * 4, F32, pat="p (g t) -> p g t", g=8)
    lbl = top.take(2 * NH * 4, F32)
    lb = top.take(NH * 4, F32)
    oml = top.take(NH * 4, F32)
    noml = top.take(NH * 4, F32)
    ghead = top.take(4, F32)
    cw = top.take(24 * 4, F32, pat="p (g j) -> p g j", g=8)
    cb = top.take(8 * 4, F32)
    bm_col = top.take(32 * 4, F32)
    c_f = top.take(KC * 4, F32)
    c_bf = top.take(KC * 2, BF16)
    stat = top.take(64 * 4, F32)
    work_off = top.cur
    work = Bump(work_off, ARENA_BYTES)
    wo_scr = Bump(wo_off, wo_off + KC * D * 2)
    hy_scr = Bump(0, 2 * KC * TH * 2)

    def bank(b, n=1):
        return ps[:, b * 512:(b + n) * 512]

    PB = [Buf("psb%d" % i) for i in range(8)]

    def dma(q, out, in_, key, reads=(), writes=(), group=False):
        if q == "pool":
            return sc.add("pool", lambda e, o=out, i=in_: e.dma_start(out=o, in_=i), reads, writes, dma_key=key, group=group)
        return sc.add("sp", lambda e, o=out, i=in_: e.dma_start(out=o, in_=i), reads, writes, dma_key=key, group=group)

    def mm(out, lhsT, rhs, start, stop, reads, writes):
        return sc.add("pe", lambda e, o=out, l=lhsT, r=rhs, s0=start, s1=stop: e.matmul(o, l, r, start=s0, stop=s1), reads, writes)

    def tr(out, in_, ident, reads, writes):
        return sc.add("pe", lambda e, o=out, i=in_, d=ident: e.transpose(o, i, d), reads, writes)

    def act(out, in_, func, reads, writes, bias=None, scale=None):
        kw = {}
        if bias is not None:
            kw["bias"] = bias
        if scale is not None:
            kw["scale"] = scale
        return sc.add("act", lambda e, o=out, i=in_, f=func, k=kw: e.activation(o, i, f, **k), reads, writes)

    def ts(eng, out, in0, s1, s2, op0, op1, reads, writes):
        if s2 is None:
            return sc.add(eng, lambda e, o=out, i=in0, a=s1, p0=op0: e.tensor_scalar(o, i, a, None, p0), reads, writes)
        return sc.add(eng, lambda e, o=out, i=in0, a=s1, b=s2, p0=op0, p1=op1: e.tensor_scalar(o, i, a, b, p0, p1), reads, writes)

    def tt(eng, out, in0, in1, op, reads, writes):
        return sc.add(eng, lambda e, o=out, a=in0, b=in1, p=op: e.tensor_tensor(o, a, b, p), reads, writes)

    def stt(out, in0, scalar, in1, op0, op1, reads, writes):
        return sc.add("dve", lambda e, o=out, a=in0, s=scalar, b=in1, p0=op0, p1=op1: e.scalar_tensor_tensor(o, a, s, b, p0, p1), reads, writes)

    def cp(eng, out, in_, reads, writes):
        return sc.add(eng, lambda e, o=out, i=in_: e.tensor_copy(o, i), reads, writes)

    CONST = Buf("const")
    for dst, src in ((ident_f, ident_d), (tri_f, tri_d), (smask, smask_d), (lbl, lbl_d), (ghead, ghead_d),
                     (cw.rearrange("p g j -> p (g j)"), cw_d), (cb, cb_d), (bm_col, bmg_d), (c_f, c_d)):
        dma("sp", dst, src, "const", writes=[CONST], group=True)
    bada_row = hy_scr.take(3 * D * 4, F32)
    gpre_row = hy_scr.take(D * 4, F32)
    gpost_row = hy_scr.take(D * 4, F32)
    gs_row = hy_scr.take(D * 4, F32)
    gg_row = hy_scr.take(D * 4, F32)
    dma("sp", bada_row[0:1, :], bada_d, "const", writes=[CONST], group=True)
    dma("sp", gpre_row[0:1, :], gpre_d, "const", writes=[CONST], group=True)
    dma("sp", gpost_row[0:1, :], gpost_d, "const", writes=[CONST], group=True)
    wada_slot = [wo_scr.take(KC * 512 * 2, BF16, pat="p (k n) -> p k n", k=KC) for _ in range(2)]
    WA = [Buf("wada0"), Buf("wada1")]
    mod_row = wo_scr.take(3 * D * 4, F32)
    MOD = Buf("mod")
    SMALL = Buf("small")

    sc.add("dve", lambda e: e.memset(ones_f, 1.0), writes=[SMALL])
    sc.add("dve", lambda e: e.memset(Sst.rearrange("p h v -> p (h v)"), 0.0), writes=[SMALL])
    sc.add("dve", lambda e: e.memset(halo.rearrange("p g t -> p (g t)"), 0.0), writes=[SMALL])
    cp("dve", ident_b, ident_f, [CONST], [SMALL])
    act(c_bf, c_f, AF.Silu, [CONST], [SMALL])
    tt("dve", lb, lbl[:, 0:NH], lbl[:, NH:2 * NH], ALU.subtract, [CONST], [SMALL])
    act(lb, lb, AF.Sigmoid, [SMALL], [SMALL])
    ts("dve", oml, lb, -1.0, 1.0, ALU.mult, ALU.add, [SMALL], [SMALL])
    ts("dve", noml, oml, -1.0, None, ALU.mult, None, [SMALL], [SMALL])

    for n in range(12):
        s = n % 2
        dma("pool", wada_slot[s].rearrange("p k n -> p (k n)"), wada_d[n], "wada%d" % s, writes=[WA[s]])
        for k in range(KC):
            mm(bank(s)[0:1, :], c_bf[:, k:k + 1], wada_slot[s][:, k, :], k == 0, k == KC - 1,
               [SMALL, WA[s]], [PB[s]])
        tt("dve", mod_row[0:1, n * 512:(n + 1) * 512], bank(s)[0:1, :], bada_row[0:1, n * 512:(n + 1) * 512],
           ALU.add, [PB[s], CONST], [MOD])
    shift_row = mod_row[0:1, 0:D]
    scale_row = mod_row[0:1, D:2 * D]
    gate_row = mod_row[0:1, 2 * D:3 * D]
    stt(gs_row[0:1, :], scale_row, 1.0, gpre_row[0:1, :], ALU.add, ALU.mult, [MOD, CONST], [MOD])
    tt("dve", gg_row[0:1, :], gate_row, gpost_row[0:1, :], ALU.mult, [MOD, CONST], [MOD])
    for k in range(KC):
        mm(bank(2)[:, k:k + 1], gs_row[0:1, k * 128:(k + 1) * 128], ones_f[0:1, 0:1], True, True, [MOD, SMALL], [PB[2]])
    for k in range(KC):
        mm(bank(2)[:, 16 + k:17 + k], shift_row[0:1, k * 128:(k + 1) * 128], ones_f[0:1, 0:1], True, True,
           [MOD, SMALL], [PB[2]])
    cp("dve", gsh_col, bank(2)[:, 0:32], [PB[2]], [SMALL])
    for n in range(4):
        mm(bank(4 + n), ones_f[0:1, :], gg_row[0:1, n * 512:(n + 1) * 512], True, True, [MOD, SMALL], [PB[4 + n]])
    cp("dve", gg_bc, ps[:, 4 * 512:8 * 512], [PB[4], PB[5], PB[6], PB[7]], [SMALL])
    sc.barrier()

    if "gsh" in dbg_d:
        dma("sp", dbg_d["gsh"], gsh_col, "dbg_gsh", reads=[SMALL])
    if "gg" in dbg_d:
        dma("sp", dbg_d["gg"], gg_bc, "dbg_gg", reads=[SMALL])

    WOB = Buf("wo")

    def load_wo():
        for q in range(4):
            dma("pool", WO[:, 4 * q:4 * q + 4, :].rearrange("p k n -> p (k n)"),
                wo_d[:, 4 * q * D:(4 * q + 4) * D], "wo", writes=[WOB], group=True)

    ring_off = work.cur
    ring = [work.take(KC * 128 * 2, BF16, pat="p (k n) -> p k n", k=KC) for _ in range(NRING)]
    RB = [Buf("ring%d" % i) for i in range(NRING)]
    ring_pos = [0]
    phase_off = work.cur

    def wload(src_list):
        i = ring_pos[0] % NRING
        ring_pos[0] += 1
        for (k0, k1, src) in src_list:
            dma("pool", ring[i][:, k0:k1, :].rearrange("p k n -> p (k n)"), src, "ring%d" % i, writes=[RB[i]])
        return ring[i], RB[i]

    HT = [Buf("hT%d" % t) for t in range(NTT)]
    YT = [Buf("yT%d" % g) for g in range(16)]

    for h in range(NHALF):
        t0 = h * TH
        work.reset(phase_off)
        xt = [work.take(D * 4, F32) for _ in range(2)]
        xn = [work.take(D * 4, F32) for _ in range(2)]
        junk = work.take(D * 2, BF16)
        XT = [Buf("xt0"), Buf("xt1")]
        XN = [Buf("xn0"), Buf("xn1")]
        JK = Buf("junk")
        ST = [Buf("st0"), Buf("st1")]
        def p1_stage_a(tb):
            s = tb % 2
            r0 = t0 + tb * 128
            dma("sp", xt[s], x_d[r0:r0 + 128, :], "xt%d" % s, writes=[XT[s]])
            ssq = stat[:, 4 * s:4 * s + 1]
            rt = stat[:, 4 * s + 1:4 * s + 2]
            rstd = stat[:, 4 * s + 2:4 * s + 3]
            sc.add("act", lambda e, o=junk, i=xt[s], a=ssq: e.activation(o, i, AF.Square, accum_out=a), [XT[s]], [JK, ST[s]])
            act(rt, ssq, AF.Sqrt, [ST[s]], [ST[s]], bias=EPS, scale=1.0 / D)
            sc.add("dve", lambda e, o=rstd, i=rt: e.reciprocal(o, i), [ST[s]], [ST[s]])
            ts("dve", xn[s], xt[s], rstd, None, ALU.mult, None, [XT[s], ST[s]], [XN[s]])

        def p1_stage_b(tb):
            s = tb % 2
            pb = [PB[4 * s + i] for i in range(4)]
            for k in range(KC):
                tr(ps[:, (4 * s) * 512 + k * 128:(4 * s) * 512 + (k + 1) * 128], xn[s][:, k * 128:(k + 1) * 128], ident_f,
                   [XN[s], SMALL], [pb[k // 4]])
            for k in range(KC):
                src = ps[:, (4 * s) * 512 + k * 128:(4 * s) * 512 + (k + 1) * 128]
                dst = hT[:, k, tb * 128:(tb + 1) * 128]
                if k % 2 == 0:
                    act(dst, src, AF.Identity, [pb[k // 4], SMALL], [HT[tb // 4]],
                        bias=gsh_col[:, 16 + k:17 + k], scale=gsh_col[:, k:k + 1])
                else:
                    ts("dve", dst, src, gsh_col[:, k:k + 1], gsh_col[:, 16 + k:17 + k], ALU.mult, ALU.add,
                       [pb[k // 4], SMALL], [HT[tb // 4]])

        for tb in range(NTB + 1):
            if tb < NTB:
                p1_stage_a(tb)
            if tb >= 1:
                p1_stage_b(tb - 1)
        sc.barrier()
        if h == 0 and "hT" in dbg_d:
            dma("sp", dbg_d["hT"], hT.rearrange("p k t -> p (k t)"), "dbg_hT", reads=HT)

        work.reset(phase_off)
        T = [work.take((TH + 2) * 4, F32) for _ in range(5)]
        Ubuf = T[1]
        T = [t_[:, 0:TH] for t_ in T]
        TB_ = [Buf("T%d" % i) for i in range(5)]
        QR = work.take(TH * 2, BF16)
        KR = work.take(TH * 2, BF16)
        KTm = work.take(TH * 2, BF16, pat="p (c d) -> p c d", c=NTB)
        VTm = work.take(TH * 2, BF16, pat="p (c d) -> p c d", c=NTB)
        STm = work.take(TH * 2, BF16, pat="p (c d) -> p c d", c=NTB)
        SBs = [work.take(128 * 2, BF16) for _ in range(NTB)]
        TMP = work.take(128 * 4, F32)
        cst = work.take(5 * NTB * 4, F32)
        QRb, KRb, KTb, VTb, STb = Buf("QR"), Buf("KR"), Buf("KT"), Buf("VT"), Buf("ST")
        SBb = [Buf("SB%d" % i) for i in range(NTB)]
        TMPb, CSTb, SSb = Buf("TMP"), Buf("cst"), Buf("S")
        bmid = cst[:, 0:NTB]
        blast = cst[:, NTB:2 * NTB]
        emid = cst[:, 2 * NTB:3 * NTB]
        Edec = cst[:, 3 * NTB:4 * NTB]
        dec = cst[:, 4 * NTB:5 * NTB]
        psK = bank(7).bitcast(BF16)

        for hd in range(NH):
            if h == 0 and hd == 2:
                load_wo()
            Wf, Bf = wload([(0, KC, win_d[8 + hd])])
            Wq, Bq = wload([(0, KC, win_d[hd])])
            Wz, Bz = wload([(0, KC, win_d[24 + hd])])
            Wi, Bi = wload([(0, KC, win_d[16 + hd])])
            col = slice(hd, hd + 1)

            def proj_fm(W, Bw, b0):
                for t in range(NTT):
                    for k in range(KC):
                        mm(bank(b0 + t), W[:, k, :], hT[:, k, t * 512:(t + 1) * 512], k == 0, k == KC - 1,
                           [Bw, HT[t]], [PB[b0 + t]])

            proj_fm(Wf, Bf, 0)
            act(T[0], bank(0, 2), AF.Sigmoid, [PB[0], PB[1]], [TB_[0]])
            proj_fm(Wq, Bq, 2)
            act(T[4], bank(2, 2), AF.Copy, [PB[2], PB[3]], [TB_[4]])
            act(T[1], T[0], AF.Ln, [TB_[0], SMALL], [TB_[1]], bias=lb[:, col], scale=oml[:, col])
            ts("dve", T[3], T[0], noml[:, col], oml[:, col], ALU.mult, ALU.add, [TB_[0], SMALL], [TB_[3]])
            sc.add("dve", lambda e, o=T[2], i=T[1]: e.tensor_tensor_scan(o, smask, i, 0.0, ALU.mult, ALU.add),
                   [TB_[1], CONST], [TB_[2]])
            b3 = T[2].rearrange("p (c t) -> p c t", c=NTB)
            cp("dve", bmid.unsqueeze(2), b3[:, :, 63:64], [TB_[2]], [CSTb])
            cp("dve", blast.unsqueeze(2), b3[:, :, 127:128], [TB_[2]], [CSTb])
            tt("dve", T[1].rearrange("p (c t) -> p c t", c=NTB), b3, b3[:, :, 63:64].broadcast_to([P, NTB, 128]),
               ALU.subtract, [TB_[2]], [TB_[1]])
            act(T[0], T[1], AF.Exp, [TB_[1]], [TB_[0]])
            act(T[2], T[1], AF.Exp, [TB_[1], CSTb], [TB_[2]], scale=-1.0)
            act(emid, bmid, AF.Exp, [CSTb], [CSTb])
            act(dec, blast, AF.Exp, [CSTb], [CSTb])
            tt("dve", Edec, blast, bmid, ALU.subtract, [CSTb], [CSTb])
            act(Edec, Edec, AF.Exp, [CSTb], [CSTb])
            tt("dve", QR, T[4], T[0], ALU.mult, [TB_[4], TB_[0]], [QRb])
            tt("dve", KR, T[3], T[2], ALU.mult, [TB_[3], TB_[2]], [KRb])
            proj_fm(Wz, Bz, 0)
            act(T[4], bank(0, 2), AF.Silu, [PB[0], PB[1]], [TB_[4]])
            for c in range(NTB):
                for k in range(KC):
                    mm(ps[:, 2 * 512 + c * 128:2 * 512 + (c + 1) * 128], hT[:, k, c * 128:(c + 1) * 128], Wi[:, k, :],
                       k == 0, k == KC - 1, [Bi, HT[c // 4]], [PB[2 + c // 4]])
            cp("dve", VTm.rearrange("p c d -> p (c d)"), bank(2, 2), [PB[2], PB[3]], [VTb])
            for c in range(NTB):
                tr(psK[:, c * 128:(c + 1) * 128], KR[:, c * 128:(c + 1) * 128], ident_b, [KRb, SMALL], [PB[7]])
            act(KTm.rearrange("p c d -> p (c d)"), psK, AF.Copy, [PB[7]], [KTb])
            for c in range(NTB):
                mm(ps[:, 4 * 512 + c * 128:4 * 512 + (c + 1) * 128], KR[:, c * 128:(c + 1) * 128],
                   QR[:, c * 128:(c + 1) * 128], True, True, [KRb, QRb], [PB[4 + c // 4]])
            tt("dve", STm, bank(4, 2).rearrange("p (c t) -> p c t", c=NTB),
               tri_f.unsqueeze(1).broadcast_to([P, NTB, 128]), ALU.mult, [PB[4], PB[5], CONST], [STb])
            for c in range(NTB):
                mm(ps[:, 6 * 512 + c * 128:6 * 512 + (c + 1) * 128], KTm[:, c, :], VTm[:, c, :], True, True,
                   [KTb, VTb], [PB[6 + c // 4]])
            for c in range(NTB):
                sb = c % 2
                ts("dve", SBs[sb], Sst[:, hd, :], emid[:, c:c + 1], None, ALU.mult, None, [SSb, CSTb], [SBb[sb]])
                o_ap = ps[:, 4 * 512 + c * 128:4 * 512 + (c + 1) * 128]
                mm(o_ap, VTm[:, c, :], STm[:, c, :], True, False, [VTb, STb], [PB[4 + c // 4]])
                mm(o_ap, SBs[sb], QR[:, c * 128:(c + 1) * 128], False, True, [SBb[sb], QRb], [PB[4 + c // 4]])
                ts("dve", TMP, ps[:, 6 * 512 + c * 128:6 * 512 + (c + 1) * 128], Edec[:, c:c + 1], None, ALU.mult, None,
                   [PB[6 + c // 4], CSTb], [TMPb])
                stt(Sst[:, hd, :], Sst[:, hd, :], dec[:, c:c + 1], TMP, ALU.mult, ALU.add, [SSb, TMPb, CSTb], [SSb])
            act(T[0], bank(4, 2), AF.Square, [PB[4], PB[5]], [TB_[0]])
            for t in range(NTT):
                mm(bank(6 + t), ones_f, T[0][:, t * 512:(t + 1) * 512], True, True, [TB_[0], SMALL], [PB[6 + t]])
            act(T[1], bank(6, 2), AF.Sqrt, [PB[6], PB[7]], [TB_[1]], bias=EPS, scale=1.0 / 128)
            sc.add("dve", lambda e, o=T[1]: e.reciprocal(o, o), [TB_[1]], [TB_[1]])
            tt("dve", T[3], bank(4, 2), T[1], ALU.mult, [PB[4], PB[5], TB_[1]], [TB_[3]])
            stt(yT[:, hd, :], T[3], ghead[:, 0:1], T[4], ALU.mult, ALU.mult, [TB_[3], TB_[4], CONST], [YT[hd]])

        UB = TB_[1]
        for g in range(8):
            Wcc, Bcc = wload([(0, KC, win_d[40 + g])])
            Wvb, Bvb = wload([(0, KC, win_d[48 + g])])
            Wzb, Bzb = wload([(0, KC, win_d[56 + g])])
            Wbb, Bbb = wload([(0, KC, win_d[32 + g])])

            def proj_fm2(W, Bw, b0):
                for t in range(NTT):
                    for k in range(KC):
                        mm(bank(b0 + t), W[:, k, :], hT[:, k, t * 512:(t + 1) * 512], k == 0, k == KC - 1,
                           [Bw, HT[t]], [PB[b0 + t]])

            proj_fm2(Wcc, Bcc, 0)
            act(T[0], bank(0, 2), AF.Copy, [PB[0], PB[1]], [TB_[0]])
            proj_fm2(Wvb, Bvb, 2)
            cp("dve", Ubuf[:, 0:2], halo[:, g, :], [SSb], [UB])
            tt("dve", Ubuf[:, 2:TH + 2], bank(2, 2), T[0], ALU.mult, [PB[2], PB[3], TB_[0]], [UB])
            ts("dve", T[2], Ubuf[:, 2:TH + 2], cw[:, g, 2:3], cb[:, g:g + 1], ALU.mult, ALU.add, [UB, CONST], [TB_[2]])
            stt(T[2], Ubuf[:, 1:TH + 1], cw[:, g, 1:2], T[2], ALU.mult, ALU.add, [UB, TB_[2], CONST], [TB_[2]])
            stt(T[2], Ubuf[:, 0:TH], cw[:, g, 0:1], T[2], ALU.mult, ALU.add, [UB, TB_[2], CONST], [TB_[2]])
            cp("dve", halo[:, g, :], Ubuf[:, TH:TH + 2], [UB], [SSb])
            proj_fm2(Wzb, Bzb, 4)
            act(T[4], bank(4, 2), AF.Silu, [PB[4], PB[5]], [TB_[4]])
            proj_fm2(Wbb, Bbb, 6)
            tt("dve", T[3], bank(6, 2), T[4], ALU.mult, [PB[6], PB[7], TB_[4]], [TB_[3]])
            tt("dve", yT[:, 8 + g, :], T[3], T[2], ALU.mult, [TB_[3], TB_[2]], [YT[8 + g]])
        sc.barrier()
        if h == 0 and "yT" in dbg_d:
            dma("sp", dbg_d["yT"], yT.rearrange("p k t -> p (k t)"), "dbg_yT", reads=YT)

        work.reset(phase_off)
        mixT = work.take(KC * TH * 2, BF16, pat="p (k t) -> p k t", k=KC)
        MX = [Buf("mix%d" % t) for t in range(NTB)]
        MT = [[work.take(512 * 4, F32) for _ in range(2)] for _ in range(1)]
        MTb = [Buf("mt%d" % i) for i in range(2)]
        for j in range(16):
            WU, BU = wload([(0, 8, wupa_d[j]), (8, 16, wupb_d[j])])
            WMA, BMA = wload([(0, KC, wmg_d[j])])
            WMB, BMB = wload([(0, KC, wmg_d[16 + j])])
            for t in range(NTT):
                s = (j * NTT + t) % 2
                b0 = 4 * s
                tsl = slice(t * 512, (t + 1) * 512)
                for k in range(8):
                    mm(bank(b0), WU[:, k, :], yT[:, k, tsl], k == 0, k == 7, [BU] + YT[0:8], [PB[b0]])
                for k in range(8):
                    mm(bank(b0 + 1), WU[:, 8 + k, :], yT[:, 8 + k, tsl], k == 0, k == 7, [BU] + YT[8:16], [PB[b0 + 1]])
                for k in range(KC):
                    mm(bank(b0 + 2), WMA[:, k, :], hT[:, k, tsl], k == 0, k == KC - 1, [BMA, HT[t]], [PB[b0 + 2]])
                for k in range(KC):
                    mm(bank(b0 + 3), WMB[:, k, :], hT[:, k, tsl], k == 0, k == KC - 1, [BMB, HT[t]], [PB[b0 + 3]])
                act(MT[0][0], bank(b0 + 2), AF.Sigmoid, [PB[b0 + 2], CONST], [MTb[0]], bias=bm_col[:, j:j + 1])
                act(MT[0][1], bank(b0 + 3), AF.Sigmoid, [PB[b0 + 3], CONST], [MTb[1]], bias=bm_col[:, 16 + j:17 + j])
                tt("dve", MT[0][0], bank(b0), MT[0][0], ALU.mult, [PB[b0], MTb[0]], [MTb[0]])
                tt("dve", MT[0][1], bank(b0 + 1), MT[0][1], ALU.mult, [PB[b0 + 1], MTb[1]], [MTb[1]])
                tt("dve", mixT[:, j, tsl], MT[0][0], MT[0][1], ALU.add, [MTb[0], MTb[1]], [MX[4 * t + i] for i in range(4)])
        sc.barrier()
        if h == 0 and "mixT" in dbg_d:
            dma("sp", dbg_d["mixT"], mixT.rearrange("p k t -> p (k t)"), "dbg_mixT", reads=MX)

        work.reset(phase_off)
        work.take(KC * TH * 2, BF16)
        rb = Bump(ring_off, phase_off)
        xt = [rb.take(D * 4, F32) for _ in range(2)]
        Y = rb.take(D * 4, F32)
        XT = [Buf("xt0b"), Buf("xt1b")]
        YB = Buf("Y")
        ST = [Buf("st0b"), Buf("st1b")]
        for tb in range(NTB):
            s = tb % 2
            r0 = t0 + tb * 128
            dma("sp", xt[s], x_d[r0:r0 + 128, :], "xt%d" % s, writes=[XT[s]])
            pb = [PB[4 * s + i] for i in range(4)]
            for n in range(4):
                for k in range(KC):
                    mm(bank(4 * s + n), mixT[:, k, tb * 128:(tb + 1) * 128], WO[:, k, n * 512:(n + 1) * 512],
                       k == 0, k == KC - 1, [MX[tb], WOB], [pb[n]])
            ssq = stat[:, 4 * s:4 * s + 1]
            rt = stat[:, 4 * s + 1:4 * s + 2]
            rstd = stat[:, 4 * s + 2:4 * s + 3]
            o_ps = ps[:, 4 * s * 512:(4 * s + 4) * 512]
            sc.add("act", lambda e, o=Y, i=o_ps, a=ssq: e.activation(o, i, AF.Square, accum_out=a), pb, [YB, ST[s]])
            act(rt, ssq, AF.Sqrt, [ST[s]], [ST[s]], bias=EPS, scale=1.0 / D)
            sc.add("dve", lambda e, o=rstd, i=rt: e.reciprocal(o, i), [ST[s]], [ST[s]])
            stt(Y, o_ps, rstd, gg_bc, ALU.mult, ALU.mult, pb + [ST[s], SMALL], [YB])
            tt("dve", xt[s], Y, xt[s], ALU.add, [YB, XT[s]], [XT[s]])
            dma("sp", out_d[r0:r0 + 128, :], xt[s], "ot%d" % s, reads=[XT[s]])
        sc.barrier()

    sc.barrier()

    import contextlib
    with contextlib.ExitStack() as es:
        eng_sems = {e: es.enter_context(nc.semaphore("sem_" + e)) for e in ("pe", "act", "dve", "pool")}
        dma_sems = {k: es.enter_context(nc.semaphore("dsem_" + k)) for k in sc.dma_tot}
        engines = {"pe": nc.tensor, "act": nc.scalar, "dve": nc.vector, "pool": nc.gpsimd, "sp": nc.sync}
        block = es.enter_context(nc.Block())

        @block.tensor
        def _(e):
            sc.emit_one(nc, "pe", e, eng_sems, dma_sems)

        @block.scalar
        def _(e):
            sc.emit_one(nc, "act", e, eng_sems, dma_sems)

        @block.vector
        def _(e):
            sc.emit_one(nc, "dve", e, eng_sems, dma_sems)

        @block.gpsimd
        def _(e):
            sc.emit_one(nc, "pool", e, eng_sems, dma_sems)

        @block.sync
        def _(e):
            sc.emit_one(nc, "sp", e, eng_sems, dma_sems)

    psum_cm.__exit__(None, None, None)
    arena_cm.__exit__(None, None, None)
    return nc


def _prepare(sc):
    for e in ENGS:
        for op in sc.ops[e]:
            for d in op.deps:
                if d.dma_key is None and d.fn is not None:
                    if d.eng == "pe" and e == "pe":
                        continue
                    d.inc = True
    for e in ENGS:
        c = 0
        for op in sc.ops[e]:
            if op.dma_key is None and op.inc:
                c += 1
            op.cnt = c


def _emit_one(self, nc, e, eng, eng_sems, dma_sems):
    if not getattr(self, "_prepared", False):
        _prepare(self)
        self._prepared = True
    waited = {}
    for op in self.ops[e]:
        need = {}
        for d in op.deps:
            if d.dma_key is not None:
                if d.dma_key == op.dma_key and d.dma_key in self.dma_group:
                    continue
                key = ("dma", d.dma_key)
                v = self.dma_tot[d.dma_key] if d.dma_key in self.dma_group else d.dma_val
            else:
                if d.eng == e and e == "pe":
                    continue
                if d.eng == "sp":
                    continue
                key = ("eng", d.eng)
                v = d.cnt
            if v > need.get(key, 0):
                need[key] = v
        for key, v in need.items():
            if v <= 0 or waited.get(key, 0) >= v:
                continue
            sem = dma_sems[key[1]] if key[0] == "dma" else eng_sems[key[1]]
            eng.wait_ge(sem, v)
            waited[key] = v
        if op.fn is None:
            continue
        ins = op.fn(eng)
        if op.dma_key is not None:
            ins.then_inc(dma_sems[op.dma_key], 16)
        elif op.inc:
            ins.then_inc(eng_sems[e], 1)


Sched.emit_one = _emit_one


def _layout_inputs(x, c, w_ada, b_ada, g_pre, w_in, lb_logits, g_head_a, conv_w, conv_b,
                   w_up_a, w_up_b, w_merge, b_merge, w_o, g_post):
    f = lambda a: np.ascontiguousarray(np.asarray(a, dtype=np.float32))
    shared = {
        "w_ada": f(np.asarray(w_ada)[0].reshape(KC, P, 12, 512).transpose(2, 1, 0, 3).reshape(12, P, KC * 512)),
        "b_ada": f(np.asarray(b_ada)[0].reshape(1, 3 * D)),
        "g_pre": f(np.asarray(g_pre)[0].reshape(1, D)),
        "g_post": f(np.asarray(g_post)[0].reshape(1, D)),
        "w_in": f(np.asarray(w_in)[0].reshape(KC, P, 64, 128).transpose(2, 1, 0, 3).reshape(64, P, KC * 128)),
        "lbl": f(np.asarray(lb_logits).reshape(2, NH, P).transpose(2, 0, 1).reshape(P, 2 * NH)),
        "g_head": f(np.asarray(g_head_a)[0].reshape(P, 1)),
        "conv_w": f(np.asarray(conv_w)[0].reshape(3, 8, P).transpose(2, 1, 0).reshape(P, 24)),
        "conv_b": f(np.asarray(conv_b)[0].reshape(8, P).T),
        "w_up_a": f(np.asarray(w_up_a)[0].reshape(8, P, 16, 128).transpose(2, 1, 0, 3).reshape(16, P, 8 * 128)),
        "w_up_b": f(np.asarray(w_up_b)[0].reshape(8, P, 16, 128).transpose(2, 1, 0, 3).reshape(16, P, 8 * 128)),
        "w_merge": f(np.asarray(w_merge)[0].reshape(KC, P, 32, 128).transpose(2, 1, 0, 3).reshape(32, P, KC * 128)),
        "b_merge": f(np.asarray(b_merge)[0].reshape(32, P).T),
        "w_o": f(np.asarray(w_o)[0].reshape(KC, P, D).transpose(1, 0, 2).reshape(P, KC * D)),
        "ident": np.eye(P, dtype=np.float32),
        "tri": np.triu(np.ones((P, P), dtype=np.float32)),
        "smask": f(np.tile((np.arange(128) != 0).astype(np.float32), (P, TH // 128))),
    }
    xs = np.asarray(x, dtype=np.float32)
    cs = np.asarray(c, dtype=np.float32)
    in_maps = []
    for b in range(8):
        m = dict(shared)
        m["x"] = np.ascontiguousarray(xs[b])
        m["c_col"] = f(cs[b].reshape(KC, P).T)
        in_maps.append(m)
    return in_maps


_NC_CACHE = {}


def kernel(**inputs):
    in_maps = _layout_inputs(**inputs)
    key = tuple(sorted(DEBUG.keys()))
    if key not in _NC_CACHE:
        _NC_CACHE[key] = build_program(DEBUG)
    nc = _NC_CACHE[key]
    res = run_bass_kernel_spmd(nc, in_maps, core_ids=list(range(8)))
    if DEBUG:
        kernel.last_results = res.results
    return np.stack([np.asarray(r["out"], dtype=np.float32) for r in res.results], axis=0)
```
